# Optimizing a Trainium2 kernel written in Bass

```python
import jax, jax.numpy as jnp
from jax import lax
import numpy as np

D_MODEL = 1024
BATCH = 4
SEQ = 8192
DEPTH = 1

D_CONV = D_MODEL
CONV_A_WIDTH = 3
EXPAND = 2
D_INNER = EXPAND * D_MODEL
HEAD_DIM = 64
N_HEADS = D_INNER // HEAD_DIM
N_GROUPS = 4
D_STATE = 128
SSD_CONV_WIDTH = 4
CHUNK = 128
D_XBC = D_INNER + 2 * N_GROUPS * D_STATE
D_FF = 2816
FFN_CONV_WIDTH = 3
N_IN = 2 * D_MODEL + 3 * D_CONV + D_INNER + D_XBC + N_HEADS
EPS = 1e-5
DT_MIN = 1e-3
DT_MAX = 1e-1

kernel_name = "hybrid_shortconv_ssd_gated_merge_convffn"


def rmsnorm(x, w):
    xf = x.astype(jnp.float32)
    y = xf * lax.rsqrt(jnp.mean(xf * xf, axis=-1, keepdims=True) + EPS)
    return (y * w.astype(jnp.float32)).astype(x.dtype)


def causal_dwconv(x, w):
    k, c = w.shape
    return lax.conv_general_dilated(
        x, w[:, None, :].astype(x.dtype), window_strides=(1,), padding=[(k - 1, 0)],
        dimension_numbers=("NWC", "WIO", "NWC"), feature_group_count=c)


def ssd_chunked_scan(xh, dt, a, bmat, cmat):
    b, s, h, p = xh.shape
    g, n = bmat.shape[-2:]
    j = h // g
    c = s // CHUNK
    x_ = (xh.astype(jnp.float32) * dt[..., None]).reshape(b, c, CHUNK, g, j, p)
    log_a = jnp.moveaxis((dt * a).reshape(b, c, CHUNK, g, j), 2, -1)
    bc = bmat.astype(jnp.float32).reshape(b, c, CHUNK, g, n)
    cc = cmat.astype(jnp.float32).reshape(b, c, CHUNK, g, n)
    a_cum = jnp.cumsum(log_a, axis=-1)

    causal = jnp.tril(jnp.ones((CHUNK, CHUNK), dtype=bool))
    seg = a_cum[..., :, None] - a_cum[..., None, :]
    decay_in = jnp.exp(jnp.where(causal, seg, -jnp.inf))
    cb = jnp.einsum("bclgn,bcsgn->bcgls", cc, bc)
    y_diag = jnp.einsum("bcgjls,bcsgjp->bclgjp", cb[:, :, :, None] * decay_in, x_)

    decay_to_end = jnp.exp(a_cum[..., -1:] - a_cum)
    states = jnp.einsum("bclgn,bcgjl,bclgjp->bcgjpn", bc, decay_to_end, x_)
    chunk_decay = jnp.exp(a_cum[..., -1])

    def step(carry, inp):
        st, dec = inp
        return carry * dec[..., None, None] + st, carry

    init = jnp.zeros((b, g, j, p, n), jnp.float32)
    _, prev = lax.scan(step, init, (jnp.moveaxis(states, 1, 0), jnp.moveaxis(chunk_decay, 1, 0)))
    prev = jnp.moveaxis(prev, 0, 1)
    y_off = jnp.einsum("bclgn,bcgjpn,bcgjl->bclgjp", cc, prev, jnp.exp(a_cum))
    return (y_diag + y_off).reshape(b, s, h, p)


def gated_group_rmsnorm(y, z, w):
    bsz, s, d = y.shape
    yf = (y.astype(jnp.float32) * jax.nn.silu(z.astype(jnp.float32))).reshape(bsz, s, N_GROUPS, d // N_GROUPS)
    yf = yf * lax.rsqrt(jnp.mean(yf * yf, axis=-1, keepdims=True) + EPS)
    return (yf.reshape(bsz, s, d) * w.astype(jnp.float32)).astype(y.dtype)


def hybrid_mixer(u, w_in, conv_a_w, w_a_out, ssd_conv_w, ssd_conv_b, dt_bias, a_log,
                 d_skip, ssd_norm_w, w_s_out, w_o):
    bsz, s, _ = u.shape
    proj = u @ w_in
    sizes = [D_MODEL, D_MODEL, D_CONV, D_CONV, D_CONV, D_INNER, D_XBC, N_HEADS]
    offsets = np.cumsum(sizes)[:-1].tolist()
    gate_a, gate_s, b_a, c_a, v_a, z, xbc, dt_raw = jnp.split(proj, offsets, axis=-1)

    y_a = (b_a * causal_dwconv(c_a * v_a, conv_a_w)) @ w_a_out

    xbc = jax.nn.silu(causal_dwconv(xbc, ssd_conv_w) + ssd_conv_b)
    xs, bs, cs = jnp.split(xbc, [D_INNER, D_INNER + N_GROUPS * D_STATE], axis=-1)
    xh = xs.reshape(bsz, s, N_HEADS, HEAD_DIM)
    dt = jax.nn.softplus(dt_raw.astype(jnp.float32) + dt_bias.astype(jnp.float32))
    a = -jnp.exp(a_log.astype(jnp.float32))
    y = ssd_chunked_scan(xh, dt, a,
                         bs.reshape(bsz, s, N_GROUPS, D_STATE),
                         cs.reshape(bsz, s, N_GROUPS, D_STATE))
    y = y + d_skip.astype(jnp.float32)[:, None] * xh.astype(jnp.float32)
    y = y.reshape(bsz, s, D_INNER).astype(u.dtype)
    y_s = gated_group_rmsnorm(y, z, ssd_norm_w) @ w_s_out

    merged = jax.nn.sigmoid(gate_a) * y_a + jax.nn.sigmoid(gate_s) * y_s
    return merged @ w_o


def conv_gated_mlp(v, w_up, ffn_conv_w, ffn_conv_b, w_down):
    hv = v @ w_up
    h1, h3 = jnp.split(hv, 2, axis=-1)
    h1 = causal_dwconv(h1, ffn_conv_w) + ffn_conv_b
    return (jax.nn.silu(h1) * h3) @ w_down


def setup_inputs(seed: int = 0) -> dict:
    key = jax.random.key(seed)
    ks = jax.random.split(key, 24)
    f32 = jnp.float32

    def nrm(k, shape, scale):
        return jax.random.normal(k, shape, f32) * scale

    dt0 = jnp.exp(jax.random.uniform(ks[9], (DEPTH, N_HEADS), f32)
                  * (np.log(DT_MAX) - np.log(DT_MIN)) + np.log(DT_MIN))
    dt_bias = dt0 + jnp.log(-jnp.expm1(-dt0))
    return {
        "x": nrm(ks[0], (BATCH, SEQ, D_MODEL), 1.0),
        "norm_mix_w": 1.0 + nrm(ks[1], (DEPTH, D_MODEL), 0.02),
        "w_in": nrm(ks[2], (DEPTH, D_MODEL, N_IN), D_MODEL ** -0.5),
        "conv_a_w": nrm(ks[3], (DEPTH, CONV_A_WIDTH, D_CONV), CONV_A_WIDTH ** -0.5),
        "w_a_out": nrm(ks[4], (DEPTH, D_CONV, D_MODEL), D_CONV ** -0.5),
        "ssd_conv_w": nrm(ks[5], (DEPTH, SSD_CONV_WIDTH, D_XBC), SSD_CONV_WIDTH ** -0.5),
        "ssd_conv_b": nrm(ks[6], (DEPTH, D_XBC), 0.02),
        "dt_bias": dt_bias,
        "a_log": jnp.log(jax.random.uniform(ks[7], (DEPTH, N_HEADS), f32, 1.0, 16.0)),
        "d_skip": 1.0 + nrm(ks[8], (DEPTH, N_HEADS), 0.02),
        "ssd_norm_w": 1.0 + nrm(ks[10], (DEPTH, D_INNER), 0.02),
        "w_s_out": nrm(ks[11], (DEPTH, D_INNER, D_MODEL), D_INNER ** -0.5),
        "w_o": nrm(ks[12], (DEPTH, D_MODEL, D_MODEL), D_MODEL ** -0.5),
        "norm_ffn_w": 1.0 + nrm(ks[13], (DEPTH, D_MODEL), 0.02),
        "w_up": nrm(ks[14], (DEPTH, D_MODEL, 2 * D_FF), D_MODEL ** -0.5),
        "ffn_conv_w": nrm(ks[15], (DEPTH, FFN_CONV_WIDTH, D_FF), FFN_CONV_WIDTH ** -0.5),
        "ffn_conv_b": nrm(ks[16], (DEPTH, D_FF), 0.02),
        "w_down": nrm(ks[17], (DEPTH, D_FF, D_MODEL), D_FF ** -0.5),
        "final_norm_w": 1.0 + nrm(ks[18], (D_MODEL,), 0.02),
    }


def reference(x, norm_mix_w, w_in, conv_a_w, w_a_out, ssd_conv_w, ssd_conv_b, dt_bias,
              a_log, d_skip, ssd_norm_w, w_s_out, w_o, norm_ffn_w, w_up, ffn_conv_w,
              ffn_conv_b, w_down, final_norm_w):
    h = x
    for l in range(DEPTH):
        u = rmsnorm(h, norm_mix_w[l])
        h = h + hybrid_mixer(u, w_in[l], conv_a_w[l], w_a_out[l], ssd_conv_w[l], ssd_conv_b[l],
                             dt_bias[l], a_log[l], d_skip[l], ssd_norm_w[l], w_s_out[l], w_o[l])
        v = rmsnorm(h, norm_ffn_w[l])
        h = h + conv_gated_mlp(v, w_up[l], ffn_conv_w[l], ffn_conv_b[l], w_down[l])
    return rmsnorm(h, final_norm_w)
```

```python
import contextlib
import numpy as np
import ml_dtypes
import concourse.bass as bass
import concourse.mybir as mybir
from concourse.bass_utils import run_bass_kernel_spmd

F32 = mybir.dt.float32
BF16 = mybir.dt.bfloat16
I32 = mybir.dt.int32
AF = mybir.ActivationFunctionType
ALU = mybir.AluOpType

N_CORES = 8
D_MODEL = 1024
SEQ = 8192
BATCH = 4
TT = 512
NH = 32
D_FF = 2816
N_IN = 10272
EPS = 1e-5

C_NMW = 0
C_NFW = 8
C_SNW = 16
C_CAW = 32
C_SCW = 56
C_SCB = 152
C_FCW = 176
C_FCB = 242
C_DTB = 264
C_ALOG = 296
C_DSK = 328
C_FLAG = 360
C_WFIN = 361
C_TOT = 1385


class Prog:
    def __init__(self, nc, stack):
        self.nc = nc
        self.stack = stack
        self.ops = []
        self.prio_mode = 0

    def op(self, eng, fn, r=(), w=(), ndma=0, dkey=None):
        self.ops.append([eng, fn, tuple(r), tuple(w), ndma, dkey, False, None, None, getattr(self, 'stage', '')])

    def pe(self, fn, r=(), w=()):
        self.op('pe', fn, r, w)

    def act(self, fn, r=(), w=()):
        self.op('act', fn, r, w)

    def dve(self, fn, r=(), w=()):
        self.op('dve', fn, r, w)

    def pool(self, fn, r=(), w=()):
        self.op('pool', fn, r, w)

    def dma(self, eng, fn, r=(), w=(), ndma=1, dkey=None):
        self.op(eng, fn, r, w, ndma=ndma, dkey=dkey)

    def build(self):
        import heapq
        nc = self.nc
        ops = self.ops
        n = len(ops)
        ENG, FN, R, W, NDMA, DKEY, SIG, SIGVAL, SEM = range(9)
        last_w = {}
        readers = {}
        last_x = {}
        xkey = getattr(self, 'xkey', None)
        deps = [None] * n
        for i, o in enumerate(ops):
            d = {}
            if xkey is not None:
                xs = set()
                for k in o[R] + o[W]:
                    xb = xkey(k)
                    if xb is not None:
                        xs.add(xb)
                for xb in xs:
                    p = last_x.get(xb)
                    if p is not None:
                        if ops[p][ENG] != o[ENG]:
                            d[p] = True
                        elif p not in d:
                            d[p] = None
                    last_x[xb] = i
            tokkey = getattr(self, 'tokkey', None)
            if tokkey is not None:
                toks = set()
                for k in o[R] + o[W]:
                    t = tokkey(k)
                    if t is not None and t not in o[R] and t not in o[W]:
                        toks.add(t)
                if toks:
                    o[R] = o[R] + tuple(toks)
            for k in o[R]:
                p = last_w.get(k)
                if p is not None:
                    d[p] = True
            for k in o[W]:
                p = last_w.get(k)
                if p is not None:
                    d[p] = True
                for rd in readers.get(k, ()):
                    if rd != i and d.get(rd) is None:
                        d[rd] = False
            for k in o[R]:
                readers.setdefault(k, []).append(i)
            for k in o[W]:
                last_w[k] = i
                readers[k] = []
            deps[i] = d
        dsz = {F32: 4, BF16: 2, I32: 4}

        class _MI:
            def then_inc(self, *a, **k):
                return self

        class _Mock:
            def __init__(self):
                self.calls = []

            def __getattr__(self, name):
                def f(*a, **k):
                    self.calls.append((name, a, k))
                    return _MI()
                return f

        cost = [0.0] * n
        lat = [0.0] * n
        for i, o in enumerate(ops):
            m = _Mock()
            o[FN](m)
            c = 0.0
            l = 0.0
            for name, a, k in m.calls:
                outap = k.get('out', a[0] if a else None)
                shp = tuple(outap.shape)
                fsz = 1
                for v in shp[1:]:
                    fsz *= v
                if name == 'dma_start':
                    nbytes = fsz * shp[0] * dsz.get(outap.dtype, 4)
                    c += 800.0 if o[ENG] == 'pool' else 80.0
                    l = max(l, 2000.0 + nbytes / 180.0)
                elif o[ENG] == 'pe':
                    c += (max(fsz, 64) / 2.4 + 12.0) * 1.13
                elif o[ENG] == 'act':
                    c += (fsz + 230.0) / 1.2
                elif o[ENG] == 'dve':
                    c += (fsz + 70.0) / 0.96 * 1.14
                else:
                    c += 150.0 + 3.0 * fsz
            cost[i] = c
            lat[i] = l
        succ = [[] for _ in range(n)]
        npred = [0] * n
        for i in range(n):
            npred[i] = len(deps[i])
            for p in deps[i]:
                succ[p].append(i)
        engs = ('pe', 'act', 'dve', 'pool', 'sp')
        rank = [0.0] * n
        for i in range(n - 1, -1, -1):
            m = 0.0
            for sidx in succ[i]:
                if rank[sidx] > m:
                    m = rank[sidx]
            rank[i] = m + cost[i] + lat[i] + 150.0
        PRIO = self.prio_mode
        if PRIO == 0:
            pkey = list(range(n))
        else:
            pkey = [(i // PRIO, -rank[i], i) for i in range(n)]
        ready = {e: [] for e in engs}
        avail = {e: [] for e in engs}
        efree = {e: 0.0 for e in engs}
        rtime = [0.0] * n
        fin = [0.0] * n
        for i in range(n):
            if npred[i] == 0:
                heapq.heappush(ready[ops[i][ENG]], (0.0, i))
        order = {e: [] for e in engs}
        bind = [-1] * n
        self.bind = bind
        done = 0
        SEM_LAT = 150.0
        while done < n:
            best = None
            for e in engs:
                rq = ready[e]
                av = avail[e]
                while rq and rq[0][0] <= efree[e]:
                    _it = heapq.heappop(rq)
                    heapq.heappush(av, (pkey[_it[1]], _it[1]))
                if av:
                    cand = (efree[e], av[0][1], e, True)
                elif rq:
                    cand = (rq[0][0], rq[0][1], e, False)
                else:
                    continue
                if best is None or cand[:2] < best[:2]:
                    best = cand
            tstart, i, e, from_av = best
            if from_av:
                heapq.heappop(avail[e])
            else:
                heapq.heappop(ready[e])
            if from_av and order[e]:
                bind[i] = order[e][-1]
            order[e].append(i)
            efree[e] = tstart + cost[i]
            fin[i] = tstart + cost[i] + lat[i]
            done += 1
            for sidx in succ[i]:
                t = fin[i] + SEM_LAT
                if t > rtime[sidx]:
                    rtime[sidx] = t
                    bind[sidx] = i
                npred[sidx] -= 1
                if npred[sidx] == 0:
                    heapq.heappush(ready[ops[sidx][ENG]], (rtime[sidx], sidx))
        self.sim_ns = max(fin) if n else 0.0
        self.sim_fin = fin
        self.sim_cost = cost
        self.sim_lat = lat
        self.sim_busy = {e: sum(cost[i] for i in order[e]) for e in engs}
        pos = [0] * n
        for e in engs:
            for j, i in enumerate(order[e]):
                pos[i] = j
        final = [None] * n
        for i, o in enumerate(ops):
            best = {}
            for p, strong in deps[i].items():
                po = ops[p]
                if po[NDMA] == 0 and o[NDMA] == 0 and po[ENG] == o[ENG]:
                    if o[ENG] == 'pe' or strong is None:
                        continue
                gk = ('d', po[DKEY], p) if po[NDMA] else ('e', po[ENG])
                if gk not in best or pos[best[gk]] < pos[p]:
                    best[gk] = p
            final[i] = list(best.values())
            for p in final[i]:
                if ops[p][NDMA] == 0:
                    ops[p][SIG] = True
        engsem = {}
        for e in ('pe', 'act', 'dve', 'pool'):
            engsem[e] = self.stack.enter_context(nc.semaphore('sem_' + e))
        cnt = {e: 0 for e in engsem}
        dsem = {}
        dcnt = {}
        for e in engs:
            for i in order[e]:
                o = ops[i]
                if o[NDMA] == 0 and o[SIG]:
                    cnt[e] += 1
                    o[SEM] = engsem[e]
                    o[SIGVAL] = cnt[e]
        dma_ops = sorted((i for i in range(n) if ops[i][NDMA]), key=lambda i: (fin[i] - lat[i] - cost[i], i))
        for i in dma_ops:
            o = ops[i]
            k = o[DKEY]
            if k not in dsem:
                dsem[k] = self.stack.enter_context(nc.semaphore('dsem_%d' % len(dsem)))
                dcnt[k] = 0
            dcnt[k] += 16 * o[NDMA]
            o[SEM] = dsem[k]
            o[SIGVAL] = dcnt[k]
        self.n_sems = 4 + len(dsem)
        self.counts = dict(cnt)
        seen = {e: {} for e in engs}

        def emit(engname, e):
            sn = seen[engname]
            for i in order[engname]:
                o = ops[i]
                need = {}
                for p in final[i]:
                    po = ops[p]
                    s = po[SEM]
                    v = po[SIGVAL]
                    key = id(s)
                    if key not in need or need[key][1] < v:
                        need[key] = (s, v)
                for key, (s, v) in need.items():
                    if sn.get(key, 0) >= v:
                        continue
                    e.wait_ge(s, v)
                    sn[key] = v
                res = o[FN](e)
                if o[NDMA]:
                    if not isinstance(res, (list, tuple)):
                        res = [res]
                    assert len(res) == o[NDMA]
                    for ins in res:
                        ins.then_inc(o[SEM], 16)
                elif o[SIG]:
                    if isinstance(res, (list, tuple)):
                        res = res[-1]
                    res.then_inc(o[SEM], 1)
            if engname == 'sp':
                for k, s in dsem.items():
                    if sn.get(id(s), 0) < dcnt[k]:
                        e.wait_ge(s, dcnt[k])
                for en, s in engsem.items():
                    if cnt[en] > 0:
                        e.wait_ge(s, cnt[en])

        with nc.Block() as block:
            @block.tensor
            def _(e):
                emit('pe', e)

            @block.scalar
            def _(e):
                emit('act', e)

            @block.vector
            def _(e):
                emit('dve', e)

            @block.gpsimd
            def _(e):
                emit('pool', e)

            @block.sync
            def _(e):
                emit('sp', e)


PRIO_MODE = 4000


class _Stop(Exception):
    pass


def build_program(NT_PRE, NT_MAIN, stop=None):
    nc = bass.Bass("TRN2", target_bir_lowering=False)
    st = contextlib.ExitStack()
    with st:
        P = Prog(nc, st)
        P.prio_mode = PRIO_MODE

        def _bank_of(k):
            if isinstance(k, tuple):
                if k[0] == 'ps':
                    return k[1]
                if k[0] == 'tp':
                    return 6 + k[1]
                return None
            if isinstance(k, str) and k.startswith('ps'):
                return int(k[2])
            return None

        P.xkey = _bank_of

        _ARENA = ('xp', 'xph', 'hp', 'hph')
        _TOKA = ('va', 'cv', 'cvh', 'yain', 'sz', 'Lb', 'Gb', 'Abc')

        def _tok_of(k):
            if isinstance(k, tuple) and k[0] in _ARENA:
                return 'arena_tok'
            if isinstance(k, tuple) and k[0] in _TOKA:
                return 'tokA'
            return None

        P.tokkey = _tok_of
        try:

            stopped = []

            def ck(name):
                P.stage = name
                if stop == name:
                    stopped.append(name)
                    raise _Stop()

            def DR(name, shape, dt, kind="Internal"):
                return nc.dram_tensor(name, list(shape), dt, kind=kind).ap()

            def SB(name, shape, dt):
                return st.enter_context(nc.sbuf_tensor(name, list(shape), dt))

            def PSM(name, shape, dt):
                return st.enter_context(nc.psum_tensor(name, list(shape), dt))

            x_main = DR("x_main", [NT_MAIN * TT, D_MODEL], F32, "ExternalInput")
            x_pre = DR("x_pre", [max(NT_PRE, 1) * TT, D_MODEL], F32, "ExternalInput")
            out = DR("out", [(NT_MAIN - 1) * TT, D_MODEL], F32, "ExternalOutput")
            w_in = DR("w_in", [D_MODEL, N_IN], F32, "ExternalInput")
            w_a_out = DR("w_a_out", [1024, 1024], F32, "ExternalInput")
            w_s_out = DR("w_s_out", [2048, 1024], F32, "ExternalInput")
            w_o = DR("w_o", [1024, 1024], F32, "ExternalInput")
            w_up = DR("w_up", [1024, 2 * D_FF], F32, "ExternalInput")
            w_down = DR("w_down", [D_FF, 1024], F32, "ExternalInput")
            cpack = DR("cpack", [128, C_TOT], F32, "ExternalInput")
            cbf = DR("cbf", [128, 384], BF16, "ExternalInput")
            negmask = DR("negmask", [128, 1024], F32, "ExternalInput")
            n_chunks_total = (NT_PRE + NT_MAIN) * 4
            scr = DR("acum_scr", [n_chunks_total, 32, 128], F32)

            pieces = {}

            def piece(name, src3, nk, W):
                pieces[name] = dict(dram=DR("wbf_" + name, [128, nk, W], BF16), src=src3, nk=nk, W=W)

            w_in3 = w_in.rearrange("(k p) n -> p k n", p=128)
            for nm, c0 in (("va", 4096), ("ca", 3072), ("ba", 2048), ("z0", 5120), ("z1", 6144),
                           ("xbc0", 7168), ("xbc1", 8192), ("xbc2", 9216), ("ga", 0), ("gs", 1024)):
                piece(nm, w_in3[:, :, c0:c0 + 1024], 8, 1024)
            piece("wa", w_a_out.rearrange("(k p) n -> p k n", p=128), 8, 1024)
            ws3 = w_s_out.rearrange("(k p) n -> p k n", p=128)
            piece("ws0", ws3[:, 0:8, :], 8, 1024)
            piece("ws1", ws3[:, 8:16, :], 8, 1024)
            piece("wo", w_o.rearrange("(k p) n -> p k n", p=128), 8, 1024)
            w_up3 = w_up.rearrange("(k p) n -> p k n", p=128)
            up_pieces = []
            for half, base in (("h1", 0), ("h3", D_FF)):
                for i, (c0, wd) in enumerate(((0, 1024), (1024, 1024), (2048, 768))):
                    nm = "%s_%d" % (half, i)
                    piece(nm, w_up3[:, :, base + c0:base + c0 + wd], 8, wd)
                    up_pieces.append(nm)
            wd3 = w_down.rearrange("(k p) n -> p k n", p=128)
            for i in range(4):
                piece("wd%d" % i, wd3[:, :, i * 256:(i + 1) * 256], 22, 256)

            pre_seq = ["xbc0", "xbc1", "xbc2"]
            main_seq = ["xbc0", "xbc1", "xbc2", "va", "ca", "ba", "wa", "ga", "z0", "z1", "ws0", "ws1", "gs", "wo",
                        "h1_0", "h1_1", "h1_2", "h3_0", "h3_1", "h3_2", "wd0", "wd1", "wd2", "wd3"]
            schedule = pre_seq * NT_PRE + main_seq * NT_MAIN

            hres = SB("hres", [128, 4, 1024], F32)
            un = SB("un", [128, 2, 1024], BF16)
            uT = SB("uT", [128, 8, 512], BF16)
            arena = SB("arena", [128, 12800], BF16)
            xp = arena[:, 0:24 * 516].rearrange("p (j t) -> p j t", j=24)
            hp = arena[:, 0:22 * 514].rearrange("p (j t) -> p j t", j=22)
            xhalo = SB("xhalo", [128, 24, 4], BF16)
            cvhalo = SB("cvhalo", [128, 8, 2], BF16)
            hhalo = SB("hhalo", [128, 22, 2], BF16)
            sz = SB("sz", [128, 4, 2048], BF16)
            yaT = SB("yaT", [128, 8, 512], BF16)
            acc = SB("acc", [128, 2, 512], BF16)
            tnh = SB("tnh", [128, 2, 512], BF16)
            xT = SB("xT", [128, 2048], BF16)
            x_s = SB("x_s", [128, 2048], BF16)
            xd = SB("xd", [128, 2048], BF16)
            xD = SB("xD", [128, 2048], BF16)
            Btm = SB("Btm", [128, 512], BF16)
            Abc = SB("Abc", [128, 2, 1024], F32)
            Lb = SB("Lb", [128, 2, 1024], BF16)
            Gb = SB("Gb", [128, 2, 1024], BF16)
            CBm = SB("CBm", [128, 512], BF16)
            S = SB("S", [128, 2048], F32)
            Sbf = SB("Sbf", [128, 2048], BF16)
            yg = SB("yg", [128, 2, 512], F32)
            yn = SB("yn", [128, 2048], BF16)
            sqj = SB("sqj", [128, 1024], BF16)
            dg = SB("dg", [128, 2, 4, 128], BF16)
            wring = SB("wring", [128, 3, 8192], BF16)
            Wdt = SB("Wdt", [128, 8, 32], BF16)
            cp = SB("cp", [128, C_TOT], F32)
            cb = SB("cb", [128, 384], BF16)
            ident = cb[:, 0:128]
            Umat = cb[:, 128:256]
            ones = cb[:, 256:384]
            aneg = SB("aneg", [128, 32], F32)
            ss = SB("ss", [128, 16], F32)
            rs = SB("rs", [128, 16], F32)
            nt_f = SB("nt_f", [128, 16], F32)
            nt_i = SB("nt_i", [128, 16], I32)
            gss = SB("gss", [128, 4, 4], F32)
            grs = SB("grs", [128, 4, 4], F32)
            gt_f = SB("gt_f", [128, 4], F32)
            gt_i = SB("gt_i", [128, 4], I32)
            dtr = SB("dtr", [128, 128], F32)
            dta = SB("dta", [128, 128], F32)
            dtl = SB("dtl", [128, 128], F32)
            dtt = SB("dtt", [128, 128], F32)
            loga = SB("loga", [128, 128], F32)
            lr = SB("lr", [128, 128], F32)
            lparts = SB("lparts", [128, 3, 128], BF16)
            acum = SB("acum", [128, 4, 32], F32)
            nacum = SB("nacum", [128, 4, 32], F32)
            Ee = SB("Ee", [128, 4, 32], F32)
            dtmp = SB("dtmp", [128, 4, 32], F32)
            dte = SB("dte", [128, 4, 32], F32)
            cdec = SB("cdec", [128, 4, 32], F32)
            acT = SB("acT", [32, 4, 128], F32)

            dummy = SB("phase_tok", [128, 8], F32)

            va_lo = Lb[:].rearrange("p a (b t) -> p (a b) t", b=2)
            va_hi = Gb[:].rearrange("p a (b t) -> p (a b) t", b=2)

            def vaj(j):
                return (va_lo if j < 4 else va_hi)[:, j % 4, :]

            cv = sz[:].rearrange("p c n -> p (c n)")[:, 0:8 * 514].rearrange("p (j t) -> p j t", j=8)
            yain = Abc[:].rearrange("p a n -> p (a n)").bitcast(BF16)[:, 0:4096].rearrange("p (j t) -> p j t", j=8)

            def tokA_phase():
                P.dve(lambda e: e.memset(dummy[:, 1:2], 0.0), w=['tokA'])

            def arena_phase():
                P.dve(lambda e: e.memset(dummy[:, 0:1], 0.0), w=['arena_tok'])

            psb = [PSM("psb%d" % i, [128, 512], F32) for i in range(6)]
            tpb = [PSM("tpb%d" % i, [128, 1024], BF16) for i in range(2)]
            state = dict(bank=0, slot=0, sched=0, tp=0)

            def next_bank():
                b = state['bank']
                state['bank'] = (b + 1) % 4
                return b

            def next_tp():
                b = state['tp']
                state['tp'] = (b + 1) % 2
                return b

            P.dma('pool', lambda e: e.dma_start(out=cp[:], in_=cpack), w=['cp'], dkey='cp')
            P.dma('pool', lambda e: e.dma_start(out=cb[:], in_=cbf), w=['cb'], dkey='cb')
            P.dma('pool', lambda e: e.dma_start(out=Wdt[:], in_=w_in3[:, :, 10240:10272]), w=['Wdt'], dkey='Wdt')
            ck('setup0')
            cast_order = []
            for nm in pre_seq * (1 if NT_PRE else 0) + main_seq:
                if nm not in cast_order:
                    cast_order.append(nm)
            stg32 = sz[:].rearrange("p c n -> p (c n)").bitcast(F32)
            stgbf = yaT[:].rearrange("p j t -> p (j t)")
            cast_state = dict(q=0)

            def issue_cast(nm):
                pc = pieces[nm]
                nk, W = pc['nk'], pc['W']
                kk = max(1, 2048 // W)
                for k0 in range(0, nk, kk):
                    k1 = min(nk, k0 + kk)
                    nel = (k1 - k0) * W
                    q = cast_state['q']
                    cast_state['q'] = 1 - q
                    s32 = stg32[:, q * 2048:q * 2048 + nel]
                    sbf = stgbf[:, q * 2048:q * 2048 + nel]
                    k32 = [('sz', 2 * q), ('sz', 2 * q + 1)]
                    kbf = [('yaT', 4 * q + i) for i in range(4)]
                    P.dma('pool', (lambda s32, pc, k0, k1: lambda e: e.dma_start(
                        out=s32.rearrange("p (k n) -> p k n", k=k1 - k0), in_=pc['src'][:, k0:k1, :]))(s32, pc, k0, k1),
                        w=k32, dkey=('stg32', q))
                    P.dve((lambda s32, sbf: lambda e: e.tensor_copy(sbf, s32))(s32, sbf), r=k32, w=kbf)
                    P.dma('pool', (lambda sbf, pc, k0, k1: lambda e: e.dma_start(
                        out=pc['dram'][:, k0:k1, :], in_=sbf.rearrange("p (k n) -> p k n", k=k1 - k0)))(sbf, pc, k0, k1),
                        r=kbf, w=[('wbf', nm)], dkey=('wbf', nm))

            n_early = 3 if NT_PRE > 0 else len(cast_order)
            for nm in cast_order[:n_early]:
                issue_cast(nm)
            late_casts = list(cast_order[n_early:])
            ck('setup1')
            P.dve(lambda e: e.memset(S[:], 0.0), w=['S'])
            P.dve(lambda e: e.memset(Sbf[:], 0.0), w=['Sbf'])
            P.dve(lambda e: e.memset(xhalo[:], 0.0), w=['xhalo'])
            P.dve(lambda e: e.memset(cvhalo[:], 0.0), w=['cvhalo'])
            P.dve(lambda e: e.memset(hhalo[:], 0.0), w=['hhalo'])
            P.dve(lambda e: e.tensor_scalar(cp[:, C_SCW:C_FCB + 22], cp[:, C_SCW:C_FCB + 22], 0.5, None, ALU.mult),
                  r=['cp'], w=['cp'])
            P.act(lambda e: e.activation(aneg[:], cp[:, C_ALOG:C_ALOG + 32], AF.Exp), r=['cp'], w=['aneg'])
            P.dve(lambda e: e.tensor_scalar(aneg[:], aneg[:], -1.0, None, ALU.mult), r=['aneg'], w=['aneg'])

            ck('setup2')
            slot_of = {}

            def issue_load(si):
                if si >= len(schedule):
                    return
                nm = schedule[si]
                pc = pieces[nm]
                slot = si % 3
                slot_of[si] = slot
                n_el = pc['nk'] * pc['W']
                dst = wring[:, slot, 0:n_el].rearrange("p (k n) -> p k n", k=pc['nk'])
                P.dma('sp', (lambda dst, src: lambda e: e.dma_start(out=dst, in_=src))(dst, pc['dram']),
                      r=[('wbf', nm)], w=[('wslot', slot)], dkey=('wslot', slot))

            for si in range(3):
                issue_load(si)

            def use_piece(expect):
                si = state['sched']
                assert schedule[si] == expect, (schedule[si], expect)
                state['sched'] = si + 1
                pc = pieces[expect]
                slot = slot_of[si]
                n_el = pc['nk'] * pc['W']
                view = wring[:, slot, 0:n_el].rearrange("p (k n) -> p k n", k=pc['nk'])
                return view, ('wslot', slot), si

            def release(si):
                issue_load(si + 3)

            def xk(j):
                return [('xp', j, c) for c in range(4)]

            def rsqrt_newton(dst, src, ti, tf, kd, ks, name):
                tk = ('rsq_t', name)
                P.dve(lambda e: e.tensor_single_scalar(ti, src.bitcast(I32), 1, ALU.arith_shift_right), r=ks, w=[tk])
                P.dve(lambda e: e.tensor_scalar(dst.bitcast(I32), ti, -1.0, float(0x5f3759df), ALU.mult, ALU.add),
                      r=[tk], w=kd)
                for _ in range(3):
                    P.dve(lambda e: e.tensor_tensor(tf, src, dst, ALU.mult), r=ks + kd, w=[tk])
                    P.dve(lambda e: e.tensor_tensor(tf, tf, dst, ALU.mult), r=[tk] + kd, w=[tk])
                    P.dve(lambda e: e.tensor_scalar(tf, tf, -0.5, 1.5, ALU.mult, ALU.add), r=[tk], w=[tk])
                    P.dve(lambda e: e.tensor_tensor(dst, dst, tf, ALU.mult), r=[tk] + kd, w=kd)

            def norm_to_T(wcol, so):
                P.dve(lambda e: e.memset(ss[:, so:so + 4], 0.0), w=[('ss', so)])
                for c in range(4):
                    P.act((lambda c: lambda e: e.activation(sqj[:], hres[:, c, :], AF.Square,
                                                            accum_out=ss[:, so + c:so + c + 1]))(c),
                          r=[('hres', c), ('ss', so)], w=[('ss', so), 'sqj'])
                P.dve(lambda e: e.tensor_scalar(ss[:, so:so + 4], ss[:, so:so + 4], 1.0 / D_MODEL, EPS, ALU.mult, ALU.add),
                      r=[('ss', so)], w=[('ss', so)])
                rsqrt_newton(rs[:, so:so + 4], ss[:, so:so + 4], nt_i[:, so:so + 4], nt_f[:, so:so + 4],
                             [('rs', so)], [('ss', so)], 'n%d' % so)
                for c in range(4):
                    par = c % 2
                    P.act((lambda c, par: lambda e: e.activation(un[:, par, :], hres[:, c, :], AF.Copy,
                                                                 scale=rs[:, so + c:so + c + 1]))(c, par),
                           r=[('hres', c), ('rs', so)], w=[('un', par)])
                    tb = next_tp()
                    for k in range(8):
                        P.pe((lambda k, par, tb: lambda e: e.transpose(tpb[tb][:, k * 128:(k + 1) * 128],
                                                                       un[:, par, k * 128:(k + 1) * 128], ident))(k, par, tb),
                             r=[('un', par), 'cb'], w=[('tp', tb)])
                    for k in range(8):
                        dst = uT[:, k, c * 128:(c + 1) * 128]
                        src = tpb[tb][:, k * 128:(k + 1) * 128]
                        wsc = cp[:, wcol + k:wcol + k + 1]
                        if tb % 2 == 0:
                            P.act((lambda dst, src, wsc: lambda e: e.activation(dst, src, AF.Copy, scale=wsc))(dst, src, wsc),
                                  r=[('tp', tb), 'cp'], w=[('uT', c)])
                        else:
                            P.dve((lambda dst, src, wsc: lambda e: e.tensor_scalar(dst, src, wsc, None, ALU.mult))(dst, src, wsc),
                                  r=[('tp', tb), 'cp'], w=[('uT', c)])

            uT_keys = [('uT', c) for c in range(4)]

            def proj_fm(wv, wkey, jl, rhs_fn, rkeys, nk, cs=(0, 512)):
                b = next_bank()
                for k in range(nk):
                    P.pe((lambda k, b: lambda e: e.matmul(psb[b][:, cs[0]:cs[1]], wv[:, k, jl * 128:(jl + 1) * 128], rhs_fn(k),
                                                          start=(k == 0), stop=(k == nk - 1)))(k, b),
                         r=[wkey] + rkeys, w=[('ps', b)])
                return psb[b], ('ps', b)

            def load_x(src, ti_rows):
                for c in range(4):
                    P.dma('pool', (lambda c: lambda e: e.dma_start(out=hres[:, c, :],
                                                                   in_=src[ti_rows + c * 128:ti_rows + (c + 1) * 128, :]))(c),
                          w=[('hres', c)], dkey=('hres', c))

            def stage_xbc(nblocks_last, pe_conv):
                P.dve(lambda e: e.tensor_copy(xp[:, :, 1:4], xhalo[:, :, 1:4]), r=['xhalo'],
                      w=[('xph', j) for j in range(24)])
                for pi, nm in enumerate(("xbc0", "xbc1", "xbc2")):
                    wv, wkey, si = use_piece(nm)
                    nb = 8 if pi < 2 else nblocks_last
                    for jl in range(nb):
                        j = pi * 8 + jl
                        ps, pk = proj_fm(wv, wkey, jl, lambda k: uT[:, k, :], uT_keys, 8)
                        P.act((lambda j, ps: lambda e: e.copy(xp[:, j, 4:516], ps[:, :]))(j, ps), r=[pk], w=xk(j))
                    release(si)
                    j0, j1 = pi * 8, pi * 8 + nb
                    P.dve((lambda j0, j1: lambda e: e.tensor_copy(xhalo[:, j0:j1, 1:4], xp[:, j0:j1, 513:516]))(j0, j1),
                          r=[k_ for j in range(j0, j1) for k_ in xk(j)], w=['xhalo'])
                    for j in range(j0, j1):
                        a = j % 2
                        ak = ('acc', a)
                        tk = ('tnh', a)
                        wc = C_SCW + j * 4
                        bc = C_SCB + j
                        dq = j % 2
                        if not pe_conv:
                            P.act((lambda j, a, wc, bc: lambda e: e.activation(acc[:, a, :], xp[:, j, 1:513], AF.Identity,
                                                                               bias=cp[:, bc:bc + 1], scale=cp[:, wc:wc + 1]))(j, a, wc, bc),
                                  r=xk(j) + [('xph', j), 'cp'], w=[ak])
                            for tap in range(1, 4):
                                P.dve((lambda j, a, wc, tap: lambda e: e.scalar_tensor_tensor(
                                    acc[:, a, :], xp[:, j, 1 + tap:513 + tap], cp[:, wc + tap:wc + tap + 1], acc[:, a, :],
                                    ALU.mult, ALU.add))(j, a, wc, tap),
                                    r=xk(j) + [('xph', j), 'cp', ak], w=[ak])
                            P.act((lambda a: lambda e: e.activation(tnh[:, a, :], acc[:, a, :], AF.Tanh))(a), r=[ak], w=[tk])
                            P.dve((lambda j, a: lambda e: e.scalar_tensor_tensor(xp[:, j, 4:516], tnh[:, a, :], 1.0, acc[:, a, :],
                                                                                 ALU.add, ALU.mult))(j, a),
                                  r=[ak, tk], w=xk(j))
                            continue
                        for tap in range(4):
                            P.dve((lambda dq, tap, wc: lambda e: e.tensor_scalar(dg[:, dq, tap, :], ident, cp[:, wc + tap:wc + tap + 1],
                                                                                 None, ALU.mult))(dq, tap, wc),
                                  r=['cb', 'cp'], w=[('dg', dq)])
                        b = next_bank()
                        for tap in range(4):
                            P.pe((lambda j, dq, tap, b: lambda e: e.matmul(psb[b][:, :], dg[:, dq, tap, :],
                                                                           xp[:, j, 1 + tap:513 + tap],
                                                                           start=(tap == 0), stop=(tap == 3)))(j, dq, tap, b),
                                 r=xk(j) + [('xph', j), ('dg', dq)], w=[('ps', b)])
                        P.act((lambda a, b, bc: lambda e: e.activation(acc[:, a, :], psb[b][:, :], AF.Identity,
                                                                       bias=cp[:, bc:bc + 1]))(a, b, bc),
                              r=[('ps', b), 'cp'], w=[ak])
                        P.act((lambda a: lambda e: e.activation(tnh[:, a, :], acc[:, a, :], AF.Tanh))(a), r=[ak], w=[tk])
                        P.dve((lambda j, a: lambda e: e.scalar_tensor_tensor(xp[:, j, 4:516], tnh[:, a, :], 1.0, acc[:, a, :],
                                                                             ALU.add, ALU.mult))(j, a),
                              r=[ak, tk], w=xk(j))

            def stage_dt():
                for c in range(4):
                    for k in range(8):
                        P.pe((lambda c, k: lambda e: e.matmul(psb[5][:, c * 32:(c + 1) * 32], uT[:, k, c * 128:(c + 1) * 128],
                                                              Wdt[:, k, :], start=(k == 0), stop=(k == 7)))(c, k),
                             r=[('uT', c), 'Wdt'], w=['ps5dt'])
                dtb_b = cp[:, C_DTB:C_DTB + 32].unsqueeze(1).to_broadcast([128, 4, 32])
                dtr3 = dtr[:].rearrange("p (c h) -> p c h", c=4)
                P.dve(lambda e: e.tensor_tensor(dtr3, psb[5][:, 0:128].rearrange("p (c h) -> p c h", c=4), dtb_b, ALU.add),
                      r=['ps5dt', 'cp'], w=['dtr'])
                P.act(lambda e: e.activation(dta[:], dtr[:], AF.Abs), r=['dtr'], w=['dta'])
                P.act(lambda e: e.activation(dtl[:], dta[:], AF.Exp, scale=-1.0), r=['dta'], w=['dtl'])
                P.act(lambda e: e.activation(dtl[:], dtl[:], AF.Ln, bias=1.0), r=['dtl'], w=['dtl'])
                P.dve(lambda e: e.scalar_tensor_tensor(dtt[:], dtr[:], 0.0, dtl[:], ALU.max, ALU.add), r=['dtr', 'dtl'],
                      w=['dtt'])
                an_b = aneg[:].unsqueeze(1).to_broadcast([128, 4, 32])
                P.dve(lambda e: e.tensor_tensor(loga[:].rearrange("p (c h) -> p c h", c=4),
                                                dtt[:].rearrange("p (c h) -> p c h", c=4), an_b, ALU.mult),
                      r=['dtt', 'aneg'], w=['loga'])
                P.dve(lambda e: e.tensor_copy(lparts[:, 0, :], loga[:]), r=['loga'], w=['lp0'])
                P.dve(lambda e: e.tensor_tensor(lr[:], loga[:], lparts[:, 0, :], ALU.subtract), r=['loga', 'lp0'], w=['lr'])
                P.dve(lambda e: e.tensor_copy(lparts[:, 1, :], lr[:]), r=['lr'], w=['lp1'])
                P.dve(lambda e: e.tensor_tensor(lr[:], lr[:], lparts[:, 1, :], ALU.subtract), r=['lr', 'lp1'], w=['lr'])
                P.dve(lambda e: e.tensor_copy(lparts[:, 2, :], lr[:]), r=['lr'], w=['lp2'])

            lpk = ['lp0', 'lp1', 'lp2']

            def ssd_prep(c, gc, full):
                q = c
                for t in range(3):
                    P.pe((lambda t: lambda e: e.matmul(psb[5][:, 128:160], Umat, lparts[:, t, c * 32:(c + 1) * 32],
                                                       start=(t == 0), stop=(t == 2)))(t), r=lpk + ['cb'], w=['ps5a'])
                for t in range(3):
                    P.pe((lambda t: lambda e: e.matmul(psb[5][:, 160:192], ones, lparts[:, t, c * 32:(c + 1) * 32],
                                                       start=(t == 0), stop=(t == 2)))(t), r=lpk + ['cb'], w=['ps5t'])
                P.dve(lambda e: e.tensor_copy(acum[:, q, :], psb[5][:, 128:160]), r=['ps5a'], w=[('acum', q)])
                P.dve(lambda e: e.tensor_tensor(dtmp[:, q, :], psb[5][:, 160:192], acum[:, q, :], ALU.subtract),
                      r=['ps5t', ('acum', q)], w=[('dtmp', q)])
                P.act(lambda e: e.activation(dte[:, q, :], dtmp[:, q, :], AF.Exp), r=[('dtmp', q)], w=[('dte', q)])
                P.act(lambda e: e.activation(cdec[:, q, :], psb[5][:, 160:192], AF.Exp), r=['ps5t'], w=[('cdec', q)])
                if full:
                    for t in range(3):
                        P.pe((lambda t: lambda e: e.matmul(psb[5][0:32, 256:384], lparts[:, t, c * 32:(c + 1) * 32], Umat,
                                                           start=(t == 0), stop=(t == 2)))(t), r=lpk + ['cb'], w=['ps5T'])
                    P.dve(lambda e: e.tensor_scalar(nacum[:, q, :], psb[5][:, 128:160], -1.0, None, ALU.mult),
                          r=['ps5a'], w=[('nacum', q)])
                    P.act(lambda e: e.activation(Ee[:, q, :], psb[5][:, 128:160], AF.Exp), r=['ps5a'], w=[('Ee', q)])
                    P.act(lambda e: e.copy(acT[:, q, :], psb[5][0:32, 256:384]), r=['ps5T'], w=[('acT', q)])
                    P.dma('act', lambda e: e.dma_start(out=scr[gc], in_=acT[:, q, :]), r=[('acT', q)], w=[('scr', q)],
                          dkey=('scr', q))

            def ssd_chunk(c, gc, full):
                q = c
                cs = slice(4 + c * 128, 4 + (c + 1) * 128)
                for half in range(2):
                    tb = next_tp()
                    for jj in range(8):
                        j = half * 8 + jj
                        P.pe((lambda j, jj, tb: lambda e: e.transpose(tpb[tb][:, jj * 128:(jj + 1) * 128], xp[:, j, cs],
                                                                      ident))(j, jj, tb),
                             r=[('xp', j, c), 'cb'], w=[('tp', tb)])
                    P.act((lambda half, tb: lambda e: e.copy(xT[:, half * 1024:(half + 1) * 1024], tpb[tb][:, :]))(half, tb),
                          r=[('tp', tb)], w=[('xT', half)])
                tb = next_tp()
                for g in range(4):
                    P.pe((lambda g, tb: lambda e: e.transpose(tpb[tb][:, g * 128:(g + 1) * 128], xp[:, 16 + g, cs], ident))(g, tb),
                         r=[('xp', 16 + g, c), 'cb'], w=[('tp', tb)])
                P.act((lambda tb: lambda e: e.copy(Btm[:], tpb[tb][:, 0:512]))(tb), r=[('tp', tb)], w=['Btm'])
                xTk = [('xT', 0), ('xT', 1)]
                dt_b = dtt[:, c * 32:(c + 1) * 32].unsqueeze(2).to_broadcast([128, 32, 64])
                P.dve(lambda e: e.tensor_tensor(x_s[:].rearrange("p (h d) -> p h d", h=32),
                                                xT[:].rearrange("p (h d) -> p h d", h=32), dt_b, ALU.mult),
                      r=xTk + ['dtt'], w=['x_s'])
                dte_b = dte[:, q, :].unsqueeze(2).to_broadcast([128, 32, 64])
                P.pool(lambda e: e.tensor_tensor(xd[:].rearrange("p (h d) -> p h d", h=32),
                                                 x_s[:].rearrange("p (h d) -> p h d", h=32), dte_b, ALU.mult),
                       r=['x_s', ('dte', q)], w=['xd'])
                if full:
                    dsk_b = cp[:, C_DSK:C_DSK + 32].unsqueeze(2).to_broadcast([128, 32, 64])
                    P.pool(lambda e: e.tensor_tensor(xD[:].rearrange("p (h d) -> p h d", h=32),
                                                     xT[:].rearrange("p (h d) -> p h d", h=32), dsk_b, ALU.mult),
                           r=xTk + ['cp'], w=['xD'])
                    for g in range(4):
                        P.pe((lambda g: lambda e: e.matmul(psb[4][:, g * 128:(g + 1) * 128], xp[:, 16 + g, cs],
                                                           xp[:, 20 + g, cs], start=True, stop=True))(g),
                             r=[('xp', 16 + g, c), ('xp', 20 + g, c)], w=['ps4'])
                    P.dve(lambda e: e.tensor_tensor(CBm[:].rearrange("p (g l) -> p g l", g=4),
                                                    psb[4][:, :].rearrange("p (g l) -> p g l", g=4),
                                                    Umat.unsqueeze(1).to_broadcast([128, 4, 128]), ALU.mult),
                          r=['ps4', 'cb'], w=['CBm'])
                    P.dve(lambda e: e.memset(gss[:, q, :], 0.0), w=[('gss', q)])
                    for g in range(4):
                        gq = g % 2
                        src = scr[gc, 8 * g:8 * g + 8, :].rearrange("(o h) l -> o (h l)", o=1).partition_broadcast(128)
                        P.dma('act', (lambda gq: lambda e: e.dma_start(out=Abc[:, gq, :], in_=negmask))(gq),
                              w=[('Abc', gq)], dkey=('Abc', gq))
                        P.dma('pool', (lambda gq, src: lambda e: e.dma_start(out=Abc[:, gq, :], in_=src, accum_op=ALU.add))(gq, src),
                              r=[('scr', q)], w=[('Abc', gq)], dkey=('Abc', gq))
                        for hh in range(8):
                            h = g * 8 + hh
                            P.act((lambda gq, hh, h: lambda e: e.activation(
                                Lb[:, gq, hh * 128:(hh + 1) * 128], Abc[:, gq, hh * 128:(hh + 1) * 128], AF.Exp,
                                bias=nacum[:, q, h:h + 1]))(gq, hh, h),
                                r=[('Abc', gq), ('nacum', q)], w=[('Lb', gq)])
                        P.dve((lambda g, gq: lambda e: e.scalar_tensor_tensor(
                            Gb[:, gq, :].rearrange("p (h l) -> p h l", h=8), Lb[:, gq, :].rearrange("p (h l) -> p h l", h=8),
                            1.0, CBm[:, g * 128:(g + 1) * 128].unsqueeze(1).to_broadcast([128, 8, 128]),
                            ALU.min, ALU.mult))(g, gq),
                            r=[('Lb', gq), 'CBm'], w=[('Gb', gq)])
                        by = next_bank()
                        P.pe((lambda g, by: lambda e: e.matmul(psb[by][:, :], ident, xD[:, g * 512:(g + 1) * 512],
                                                               start=True, stop=False))(g, by),
                             r=['xD', 'cb'], w=[('ps', by)])
                        for hh in range(8):
                            h = g * 8 + hh
                            P.pe((lambda gq, hh, h, by: lambda e: e.matmul(
                                psb[by][:, hh * 64:(hh + 1) * 64], Gb[:, gq, hh * 128:(hh + 1) * 128],
                                x_s[:, h * 64:(h + 1) * 64], start=False, stop=(hh == 7)))(gq, hh, h, by),
                                r=[('Gb', gq), 'x_s'], w=[('ps', by)])
                        bo = next_bank()
                        P.pe((lambda g, bo: lambda e: e.matmul(psb[bo][:, :], xp[:, 20 + g, cs], Sbf[:, g * 512:(g + 1) * 512],
                                                               start=True, stop=True))(g, bo),
                             r=[('xp', 20 + g, c), ('Sbf', g)], w=[('ps', bo)])
                        E_b = Ee[:, q, g * 8:(g + 1) * 8].unsqueeze(2).to_broadcast([128, 8, 64])
                        P.dve((lambda gq, bo, E_b: lambda e: e.tensor_tensor(
                            yg[:, gq, :].rearrange("p (h d) -> p h d", h=8), psb[bo][:, :].rearrange("p (h d) -> p h d", h=8),
                            E_b, ALU.mult))(gq, bo, E_b),
                            r=[('ps', bo), ('Ee', q)], w=[('yg', gq)])
                        P.dve((lambda gq, by: lambda e: e.tensor_tensor(yg[:, gq, :], yg[:, gq, :], psb[by][:, :], ALU.add))(gq, by),
                              r=[('ps', by), ('yg', gq)], w=[('yg', gq)])
                        P.dve((lambda g, gq: lambda e: e.tensor_tensor(yn[:, g * 512:(g + 1) * 512], yg[:, gq, :],
                                                                       sz[:, c, g * 512:(g + 1) * 512], ALU.mult))(g, gq),
                              r=[('yg', gq), ('sz', c)], w=[('yn', g)])
                        P.act((lambda g: lambda e: e.activation(sqj[:, 0:512], yn[:, g * 512:(g + 1) * 512], AF.Square,
                                                                accum_out=gss[:, q, g:g + 1]))(g),
                              r=[('yn', g), ('gss', q)], w=[('gss', q), 'sqj'])
                for g in range(4):
                    bs = next_bank()
                    P.pe((lambda g, bs: lambda e: e.matmul(psb[bs][:, :], Btm[:, g * 128:(g + 1) * 128],
                                                           xd[:, g * 512:(g + 1) * 512], start=True, stop=True))(g, bs),
                         r=['Btm', 'xd'], w=[('ps', bs)])
                    cd_b = cdec[:, q, g * 8:(g + 1) * 8].unsqueeze(2).to_broadcast([128, 8, 64])
                    Sg = S[:, g * 512:(g + 1) * 512]
                    P.pool((lambda Sg, cd_b: lambda e: e.tensor_tensor(Sg.rearrange("p (h d) -> p h d", h=8),
                                                                       Sg.rearrange("p (h d) -> p h d", h=8), cd_b, ALU.mult))(Sg, cd_b),
                           r=[('S', g), ('cdec', q)], w=[('S', g)])
                    P.dve((lambda Sg, bs: lambda e: e.tensor_tensor(Sg, Sg, psb[bs][:, :], ALU.add))(Sg, bs),
                          r=[('S', g), ('ps', bs)], w=[('S', g)])
                    P.act((lambda g, Sg: lambda e: e.copy(Sbf[:, g * 512:(g + 1) * 512], Sg))(g, Sg),
                          r=[('S', g)], w=[('Sbf', g)])
                if full:
                    P.dve(lambda e: e.tensor_scalar(gss[:, q, :], gss[:, q, :], 1.0 / 512, 4.0 * EPS, ALU.mult, ALU.add),
                          r=[('gss', q)], w=[('gss', q)])
                    rsqrt_newton(grs[:, q, :], gss[:, q, :], gt_i[:], gt_f[:], [('grs', q)], [('gss', q)], 'g')
                    for g in range(4):
                        P.dve((lambda g: lambda e: e.tensor_scalar(yn[:, g * 512:(g + 1) * 512], yn[:, g * 512:(g + 1) * 512],
                                                                    grs[:, q, g:g + 1], None, ALU.mult))(g),
                               r=[('yn', g), ('grs', q)], w=[('yn', g)])
                    for half in range(2):
                        tb = next_tp()
                        for jj in range(8):
                            j = half * 8 + jj
                            P.pe((lambda j, jj, tb: lambda e: e.transpose(tpb[tb][:, jj * 128:(jj + 1) * 128],
                                                                          yn[:, j * 128:(j + 1) * 128], ident))(j, jj, tb),
                                 r=[('yn', j // 4), 'cb'], w=[('tp', tb)])
                        for jj in range(8):
                            j = half * 8 + jj
                            dst = xp[:, j, cs]
                            src = tpb[tb][:, jj * 128:(jj + 1) * 128]
                            wsc = cp[:, C_SNW + j:C_SNW + j + 1]
                            if tb % 2 == 0:
                                P.act((lambda dst, src, wsc: lambda e: e.activation(dst, src, AF.Copy, scale=wsc))(dst, src, wsc),
                                      r=[('tp', tb), 'cp'], w=[('xp', j, c)])
                            else:
                                P.dve((lambda dst, src, wsc: lambda e: e.tensor_scalar(dst, src, wsc, None, ALU.mult))(dst, src, wsc),
                                      r=[('tp', tb), 'cp'], w=[('xp', j, c)])

            def apply_flag():
                fl = cp[:, C_FLAG:C_FLAG + 1]
                for g in range(4):
                    Sg = S[:, g * 512:(g + 1) * 512]
                    P.dve((lambda Sg: lambda e: e.tensor_scalar(Sg, Sg, fl, None, ALU.mult))(Sg), r=[('S', g), 'cp'], w=[('S', g)])
                    P.act((lambda g, Sg: lambda e: e.copy(Sbf[:, g * 512:(g + 1) * 512], Sg))(g, Sg), r=[('S', g)], w=[('Sbf', g)])
                P.dve(lambda e: e.tensor_scalar(xhalo[:], xhalo[:], fl, None, ALU.mult), r=['xhalo', 'cp'], w=['xhalo'])
                P.dve(lambda e: e.tensor_scalar(cvhalo[:], cvhalo[:], fl, None, ALU.mult), r=['cvhalo', 'cp'], w=['cvhalo'])
                P.dve(lambda e: e.tensor_scalar(hhalo[:], hhalo[:], fl, None, ALU.mult), r=['hhalo', 'cp'], w=['hhalo'])

            def pre_tile(ti):
                norm_to_T(C_NMW, 0)
                if ti + 1 < NT_PRE:
                    load_x(x_pre, (ti + 1) * TT)
                else:
                    load_x(x_main, 0)
                for _ in range(7):
                    if late_casts:
                        issue_cast(late_casts.pop(0))
                ck('p_norm')
                if ti == 0:
                    arena_phase()
                stage_xbc(4, True)
                ck('p_xbc')
                stage_dt()
                ck('p_dt')
                for c in range(4):
                    ssd_prep(c, ti * 4 + c, False)
                ck('p_prep')
                for c in range(4):
                    ssd_chunk(c, ti * 4 + c, False)
                ck('p_ssd')

            def main_tile(ti):
                gbase = (NT_PRE + ti) * 4
                crange = range(3, 4) if ti == 0 else range(4)
                lo, hi = (384, 512) if ti == 0 else (0, 512)
                while late_casts:
                    issue_cast(late_casts.pop(0))
                if ti == 1:
                    apply_flag()
                norm_to_T(C_NMW, 0)
                ck('m_z')
                arena_phase()
                stage_xbc(8, False)
                ck('m_xbc')
                stage_dt()
                ck('m_norm')
                tokA_phase()
                wv, wkey, si = use_piece("va")
                for jl in range(8):
                    ps, pk = proj_fm(wv, wkey, jl, lambda k: uT[:, k, lo:hi], uT_keys, 8, (lo, hi))
                    P.act((lambda jl, ps: lambda e: e.copy(vaj(jl)[:, lo:hi], ps[:, lo:hi]))(jl, ps), r=[pk], w=[('va', jl)])
                release(si)
                P.dve(lambda e: e.tensor_copy(cv[:, :, 0:2], cvhalo[:]), r=['cvhalo'], w=[('cvh', j) for j in range(8)])
                wv, wkey, si = use_piece("ca")
                for jl in range(8):
                    ps, pk = proj_fm(wv, wkey, jl, lambda k: uT[:, k, lo:hi], uT_keys, 8, (lo, hi))
                    P.dve((lambda jl, ps: lambda e: e.tensor_tensor(cv[:, jl, 2 + lo:2 + hi], ps[:, lo:hi], vaj(jl)[:, lo:hi], ALU.mult))(jl, ps),
                          r=[pk, ('va', jl)], w=[('cv', jl)])
                release(si)
                P.dve(lambda e: e.tensor_copy(cvhalo[:], cv[:, :, 512:514]), r=[('cv', j) for j in range(8)], w=['cvhalo'])
                for jl in range(8):
                    wc = C_CAW + jl * 3
                    eng = P.dve
                    P.act((lambda jl, wc: lambda e: e.activation(vaj(jl)[:, lo:hi], cv[:, jl, lo:hi], AF.Copy,
                                                                 scale=cp[:, wc:wc + 1]))(jl, wc),
                        r=[('cv', jl), ('cvh', jl), 'cp'], w=[('va', jl)])
                    for tap in (1, 2):
                        eng((lambda jl, wc, tap: lambda e: e.scalar_tensor_tensor(
                            vaj(jl)[:, lo:hi], cv[:, jl, lo + tap:hi + tap], cp[:, wc + tap:wc + tap + 1], vaj(jl)[:, lo:hi],
                            ALU.mult, ALU.add))(jl, wc, tap),
                            r=[('cv', jl), ('cvh', jl), 'cp', ('va', jl)], w=[('va', jl)])
                wv, wkey, si = use_piece("ba")
                for jl in range(8):
                    ps, pk = proj_fm(wv, wkey, jl, lambda k: uT[:, k, lo:hi], uT_keys, 8, (lo, hi))
                    P.dve((lambda jl, ps: lambda e: e.tensor_tensor(yain[:, jl, lo:hi], ps[:, lo:hi], vaj(jl)[:, lo:hi], ALU.mult))(jl, ps),
                          r=[pk, ('va', jl)], w=[('yain', jl)])
                release(si)
                wv, wkey, si = use_piece("wa")
                yain_keys = [('yain', j) for j in range(8)]
                for jl in range(8):
                    ps, pk = proj_fm(wv, wkey, jl, lambda k: yain[:, k, lo:hi], yain_keys, 8, (lo, hi))
                    P.act((lambda jl, ps: lambda e: e.copy(yaT[:, jl, lo:hi], ps[:, lo:hi]))(jl, ps), r=[pk], w=[('yaT', jl)])
                release(si)
                wvA, wkeyA, siA = use_piece("ga")
                for jl in range(8):
                    ps, pk = proj_fm(wvA, wkeyA, jl, lambda k: uT[:, k, lo:hi], uT_keys, 8, (lo, hi))
                    a = jl % 2
                    P.act((lambda a, ps: lambda e: e.activation(tnh[:, a, lo:hi], ps[:, lo:hi], AF.Tanh, scale=0.5))(a, ps),
                          r=[pk], w=[('tnh', a)])
                    P.dve((lambda jl, a: lambda e: e.scalar_tensor_tensor(yaT[:, jl, lo:hi], tnh[:, a, lo:hi], 1.0, yaT[:, jl, lo:hi],
                                                                          ALU.add, ALU.mult))(jl, a),
                          r=[('tnh', a), ('yaT', jl)], w=[('yaT', jl)])
                release(siA)
                ck('m_convA')
                tokA_phase()
                for pi, nm in enumerate(("z0", "z1")):
                    wv, wkey, si = use_piece(nm)
                    for c in crange:
                        for hb in range(2):
                            b = next_bank()
                            for k in range(8):
                                P.pe((lambda c, hb, k, b, wv: lambda e: e.matmul(
                                    psb[b][:, :], uT[:, k, c * 128:(c + 1) * 128], wv[:, k, hb * 512:(hb + 1) * 512],
                                    start=(k == 0), stop=(k == 7)))(c, hb, k, b, wv),
                                    r=[wkey, ('uT', c)], w=[('ps', b)])
                            a = (c * 2 + hb) % 2
                            col0 = pi * 1024 + hb * 512
                            P.act((lambda a, b: lambda e: e.activation(tnh[:, a, :], psb[b][:, :], AF.Tanh, scale=0.5))(a, b),
                                  r=[('ps', b)], w=[('tnh', a)])
                            P.dve((lambda a, b, c, col0: lambda e: e.scalar_tensor_tensor(
                                sz[:, c, col0:col0 + 512], tnh[:, a, :], 1.0, psb[b][:, :], ALU.add, ALU.mult))(a, b, c, col0),
                                r=[('ps', b), ('tnh', a)], w=[('sz', c)])
                    release(si)
                ck('m_dt')
                for c in range(4):
                    ssd_prep(c, gbase + c, c in crange)
                ck('m_prep')
                for c in range(4):
                    ssd_chunk(c, gbase + c, c in crange)
                    ck('m_ssd%d' % c)
                ck('m_ssd')
                wv0, wkey0, si0 = use_piece("ws0")
                wv1, wkey1, si1 = use_piece("ws1")
                wvS, wkeyS, siS = use_piece("gs")
                for jl in range(8):
                    ps, pk = proj_fm(wvS, wkeyS, jl, lambda k: uT[:, k, lo:hi], uT_keys, 8, (lo, hi))
                    a = jl % 2
                    P.act((lambda a, ps: lambda e: e.activation(tnh[:, a, lo:hi], ps[:, lo:hi], AF.Tanh, scale=0.5))(a, ps),
                          r=[pk], w=[('tnh', a)])
                    b = next_bank()
                    for k in range(16):
                        wv_, wk_ = (wv0, wkey0) if k < 8 else (wv1, wkey1)
                        P.pe((lambda jl, k, b, wv_: lambda e: e.matmul(psb[b][:, lo:hi], wv_[:, k % 8, jl * 128:(jl + 1) * 128],
                                                                       xp[:, k, 4 + lo:4 + hi], start=(k == 0), stop=(k == 15)))(jl, k, b, wv_),
                             r=[wk_] + xk(k), w=[('ps', b)])
                    m = jl % 2
                    P.dve((lambda a, b, m: lambda e: e.scalar_tensor_tensor(acc[:, m, lo:hi], tnh[:, a, lo:hi], 1.0, psb[b][:, lo:hi],
                                                                            ALU.add, ALU.mult))(a, b, m),
                          r=[('tnh', a), ('ps', b)], w=[('acc', m)])
                    P.dve((lambda jl, m: lambda e: e.tensor_tensor(yaT[:, jl, lo:hi], yaT[:, jl, lo:hi], acc[:, m, lo:hi], ALU.add))(jl, m),
                           r=[('acc', m), ('yaT', jl)], w=[('yaT', jl)])
                release(si0)
                release(si1)
                release(siS)
                ck('m_merge')
                yaT_keys = [('yaT', j) for j in range(8)]
                wv, wkey, si = use_piece("wo")
                for c in crange:
                    for hb in range(2):
                        b = next_bank()
                        for k in range(8):
                            P.pe((lambda c, hb, k, b, wv: lambda e: e.matmul(psb[b][:, :], yaT[:, k, c * 128:(c + 1) * 128],
                                                                             wv[:, k, hb * 512:(hb + 1) * 512],
                                                                             start=(k == 0), stop=(k == 7)))(c, hb, k, b, wv),
                                 r=[wkey] + yaT_keys, w=[('ps', b)])
                        P.dve((lambda c, hb, b: lambda e: e.scalar_tensor_tensor(
                            hres[:, c, hb * 512:(hb + 1) * 512], psb[b][:, :], 0.5, hres[:, c, hb * 512:(hb + 1) * 512],
                            ALU.mult, ALU.add))(c, hb, b),
                            r=[('ps', b), ('hres', c)], w=[('hres', c)])
                release(si)
                ck('m_wo')
                norm_to_T(C_NFW, 4)
                arena_phase()
                P.dve(lambda e: e.tensor_copy(hp[:, :, 0:2], hhalo[:]), r=['hhalo'], w=[('hph', j) for j in range(22)])
                blk = ((0, 8), (8, 8), (16, 6))
                for pi in range(3):
                    wv, wkey, si = use_piece("h1_%d" % pi)
                    j0, nb = blk[pi]
                    for jl in range(nb):
                        j = j0 + jl
                        ps, pk = proj_fm(wv, wkey, jl, lambda k: uT[:, k, lo:hi], uT_keys, 8, (lo, hi))
                        P.act((lambda j, ps: lambda e: e.copy(hp[:, j, 2 + lo:2 + hi], ps[:, lo:hi]))(j, ps), r=[pk], w=[('hp', j)])
                    release(si)
                    P.dve((lambda j0, nb: lambda e: e.tensor_copy(hhalo[:, j0:j0 + nb, :], hp[:, j0:j0 + nb, 512:514]))(j0, nb),
                          r=[('hp', j) for j in range(j0, j0 + nb)], w=['hhalo'])
                    for j in range(j0, j0 + nb):
                        a = j % 2
                        ak = ('acc', a)
                        tk = ('tnh', a)
                        wc = C_FCW + j * 3
                        bc = C_FCB + j
                        eng = P.dve
                        P.act((lambda j, a, wc, bc: lambda e: e.activation(acc[:, a, lo:hi], hp[:, j, lo:hi], AF.Identity,
                                                                           bias=cp[:, bc:bc + 1], scale=cp[:, wc:wc + 1]))(j, a, wc, bc),
                            r=[('hp', j), ('hph', j), 'cp'], w=[ak])
                        for tap in (1, 2):
                            eng((lambda j, a, wc, tap: lambda e: e.scalar_tensor_tensor(
                                acc[:, a, lo:hi], hp[:, j, lo + tap:hi + tap], cp[:, wc + tap:wc + tap + 1], acc[:, a, lo:hi],
                                ALU.mult, ALU.add))(j, a, wc, tap),
                                r=[('hp', j), ('hph', j), 'cp', ak], w=[ak])
                        P.act((lambda a: lambda e: e.activation(tnh[:, a, lo:hi], acc[:, a, lo:hi], AF.Tanh))(a), r=[ak], w=[tk])
                        P.dve((lambda j, a: lambda e: e.scalar_tensor_tensor(hp[:, j, 2 + lo:2 + hi], tnh[:, a, lo:hi], 1.0, acc[:, a, lo:hi],
                                                                             ALU.add, ALU.mult))(j, a),
                              r=[ak, tk], w=[('hp', j)])
                for pi in range(3):
                    wv, wkey, si = use_piece("h3_%d" % pi)
                    j0, nb = blk[pi]
                    for jl in range(nb):
                        j = j0 + jl
                        ps, pk = proj_fm(wv, wkey, jl, lambda k: uT[:, k, lo:hi], uT_keys, 8, (lo, hi))
                        P.dve((lambda j, ps: lambda e: e.tensor_tensor(hp[:, j, 2 + lo:2 + hi], ps[:, lo:hi], hp[:, j, 2 + lo:2 + hi], ALU.mult))(j, ps),
                              r=[pk, ('hp', j)], w=[('hp', j)])
                    release(si)
                ck('m_up')
                hp_keys = [('hp', j) for j in range(22)]
                for q4 in range(4):
                    wv, wkey, si = use_piece("wd%d" % q4)
                    for c in crange:
                        b = next_bank()
                        for k in range(22):
                            P.pe((lambda c, k, b, wv: lambda e: e.matmul(psb[b][:, 0:256], hp[:, k, 2 + c * 128:2 + (c + 1) * 128],
                                                                         wv[:, k, :], start=(k == 0), stop=(k == 21)))(c, k, b, wv),
                                 r=[wkey] + hp_keys, w=[('ps', b)])
                        P.dve((lambda c, b, q4: lambda e: e.tensor_tensor(hres[:, c, q4 * 256:(q4 + 1) * 256], psb[b][:, 0:256],
                                                                          hres[:, c, q4 * 256:(q4 + 1) * 256], ALU.add))(c, b, q4),
                              r=[('ps', b), ('hres', c)], w=[('hres', c)])
                    release(si)
                ck('m_down')
                if ti >= 1:
                    so = 8
                    P.dve(lambda e: e.memset(ss[:, so:so + 4], 0.0), w=[('ss', so)])
                    for c in range(4):
                        P.act((lambda c: lambda e: e.activation(sqj[:], hres[:, c, :], AF.Square,
                                                                accum_out=ss[:, so + c:so + c + 1]))(c),
                              r=[('hres', c), ('ss', so)], w=[('ss', so), 'sqj'])
                    P.dve(lambda e: e.tensor_scalar(ss[:, so:so + 4], ss[:, so:so + 4], 1.0 / D_MODEL, EPS, ALU.mult, ALU.add),
                          r=[('ss', so)], w=[('ss', so)])
                    rsqrt_newton(rs[:, so:so + 4], ss[:, so:so + 4], nt_i[:, so:so + 4], nt_f[:, so:so + 4],
                                 [('rs', so)], [('ss', so)], 'n%d' % so)
                    for c in range(4):
                        if ti >= 1:
                            P.dve((lambda c: lambda e: e.scalar_tensor_tensor(hres[:, c, :], hres[:, c, :], rs[:, so + c:so + c + 1],
                                                                              cp[:, C_WFIN:C_WFIN + 1024], ALU.mult, ALU.mult))(c),
                                  r=[('hres', c), ('rs', so), 'cp'], w=[('hres', c)])
                            r0 = (ti - 1) * TT + c * 128
                            P.dma('pool', (lambda c, r0: lambda e: e.dma_start(out=out[r0:r0 + 128, :], in_=hres[:, c, :]))(c, r0),
                                  r=[('hres', c)], w=[('out', ti, c)], dkey=('outst', c))
                if ti + 1 < NT_MAIN:
                    load_x(x_main, (ti + 1) * TT)

            try:
                ck('setup')
                if NT_PRE > 0:
                    load_x(x_pre, 0)
                else:
                    load_x(x_main, 0)
                for ti in range(NT_PRE):
                    pre_tile(ti)
                for ti in range(NT_MAIN):
                    main_tile(ti)
                assert state['sched'] == len(schedule)
            except _Stop:
                pass
        except _Stop:
            pass
        P.build()
        build_program.P = P
        build_program.sbuf_free = nc.sbuf_bytes_remaining
        build_program.info = dict(n_ops=len(P.ops), n_sems=P.n_sems, counts=P.counts, sim_ms=P.sim_ns / 1e6,
                                  sim_busy={k: round(v / 1e6, 3) for k, v in P.sim_busy.items()})
    return nc


def _pack_consts(inp, flag):
    f = np.float32
    cpk = np.zeros((128, C_TOT), f)

    def fm(v):
        return np.ascontiguousarray(np.asarray(v, f).reshape(-1, 128).T)

    def fmw(wk):
        wk = np.asarray(wk, f)
        K, C = wk.shape
        return np.ascontiguousarray(wk.reshape(K, C // 128, 128).transpose(2, 1, 0).reshape(128, -1))

    cpk[:, C_NMW:C_NMW + 8] = fm(inp["norm_mix_w"][0])
    cpk[:, C_NFW:C_NFW + 8] = fm(inp["norm_ffn_w"][0])
    cpk[:, C_SNW:C_SNW + 16] = fm(inp["ssd_norm_w"][0])
    cpk[:, C_CAW:C_CAW + 24] = fmw(inp["conv_a_w"][0])
    cpk[:, C_SCW:C_SCW + 96] = fmw(inp["ssd_conv_w"][0])
    cpk[:, C_SCB:C_SCB + 24] = fm(inp["ssd_conv_b"][0])
    cpk[:, C_FCW:C_FCW + 66] = fmw(inp["ffn_conv_w"][0])
    cpk[:, C_FCB:C_FCB + 22] = fm(inp["ffn_conv_b"][0])
    cpk[:, C_DTB:C_DTB + 32] = np.broadcast_to(np.asarray(inp["dt_bias"][0], f)[None, :], (128, 32))
    cpk[:, C_ALOG:C_ALOG + 32] = np.broadcast_to(np.asarray(inp["a_log"][0], f)[None, :], (128, 32))
    cpk[:, C_DSK:C_DSK + 32] = np.broadcast_to(np.asarray(inp["d_skip"][0], f)[None, :], (128, 32))
    cpk[:, C_FLAG] = flag
    cpk[:, C_WFIN:C_WFIN + 1024] = np.broadcast_to(np.asarray(inp["final_norm_w"], f)[None, :], (128, 1024))
    return cpk


def _neg_mask():
    m = np.where(np.arange(128)[:, None] > np.arange(128)[None, :], -30000.0, 0.0).astype(np.float32)
    return np.ascontiguousarray(np.tile(m, (1, 8)))


def _const_bf16():
    c = np.zeros((128, 384), np.float32)
    c[:, 0:128] = np.eye(128)
    c[:, 128:256] = np.triu(np.ones((128, 128)))
    c[:, 256:384] = 1.0
    return c.astype(ml_dtypes.bfloat16)


_PROG_CACHE = {}


def run(inputs, seq, nt_pre, nt_main, stop=None, ncores=None):
    key = (nt_pre, nt_main, stop)
    if key not in _PROG_CACHE:
        _PROG_CACHE[key] = build_program(nt_pre, nt_main, stop)
    nc = _PROG_CACHE[key]
    x = np.asarray(inputs["x"], np.float32)
    nb = x.shape[0]
    half = (nt_main - 1) * TT
    assert seq == 2 * half and nt_pre * TT == half - TT
    cbf = _const_bf16()
    negm = _neg_mask()
    wts = {k: np.ascontiguousarray(np.asarray(inputs[k][0], np.float32))
           for k in ("w_in", "w_a_out", "w_s_out", "w_o", "w_up", "w_down")}
    in_maps = []
    for core in range(2 * nb):
        b, second = core // 2, core % 2
        if second:
            xm = x[b, half - TT:seq]
            xpre = x[b, 0:half - TT] if nt_pre else np.zeros((TT, D_MODEL), np.float32)
        else:
            xm = np.concatenate([np.zeros((TT, D_MODEL), np.float32), x[b, 0:half]], axis=0)
            xpre = np.zeros((max(nt_pre, 1) * TT, D_MODEL), np.float32)
        m = dict(x_main=np.ascontiguousarray(xm), x_pre=np.ascontiguousarray(xpre),
                 cpack=_pack_consts(inputs, float(second)), cbf=cbf, negmask=negm)
        m.update(wts)
        in_maps.append(m)
    if ncores:
        in_maps = in_maps[:ncores]
        res = run_bass_kernel_spmd(nc, in_maps, core_ids=list(range(ncores)))
        return res
    res = run_bass_kernel_spmd(nc, in_maps, core_ids=list(range(2 * nb)))
    outp = np.empty((nb, seq, D_MODEL), np.float32)
    for core in range(2 * nb):
        b, second = core // 2, core % 2
        outp[b, second * half:(second + 1) * half] = res.results[core]["out"]
    return outp


def kernel(**inputs):
    return run(inputs, SEQ, 7, 9)
```
